# Optimizing a Trainium2 kernel written in Bass

```python
import math
import jax, jax.numpy as jnp
from jax import lax
import numpy as np

D_MODEL = 1024
BATCH = 2
SEQ = 8192
DEPTH = 1

N_META = 16
BLOCK = 128
PAD = BLOCK - N_META
EPS = 1e-6
NEG = -1e30
SSD_INNER = 2 * D_MODEL
SSD_HEADDIM = 64
SSD_HEADS = SSD_INNER // SSD_HEADDIM
SSD_GROUPS = 4
SSD_HPG = SSD_HEADS // SSD_GROUPS
SSD_STATE = 128
SSD_CONV = 4
SSD_XBC = SSD_INNER + 2 * SSD_GROUPS * SSD_STATE
ATT_HEADS = 16
ATT_KV_HEADS = 2
ATT_HEADDIM = 64
ATT_GQ = ATT_HEADS // ATT_KV_HEADS
WINDOW = 128
ATT_Q = ATT_HEADS * ATT_HEADDIM
ATT_KV = ATT_KV_HEADS * ATT_HEADDIM
REL_BUCKETS = 32
REL_MAX_DIST = 128
D_FF = 2816
FFN_CONV = 3
N_BRANCH = 2
IN_WIDTHS = (SSD_INNER, SSD_XBC, SSD_HEADS, ATT_Q, ATT_KV, ATT_KV, D_MODEL, D_MODEL)
D_IN_PROJ = SSD_INNER + SSD_XBC + SSD_HEADS + ATT_Q + 2 * ATT_KV + N_BRANCH * D_MODEL

kernel_name = 'hybrid_ssd_swa_sink_convffn_block'


def rmsnorm(x, w):
    xf = x.astype(jnp.float32)
    y = xf * lax.rsqrt(jnp.mean(xf * xf, axis=-1, keepdims=True) + EPS)
    return (y * w.astype(jnp.float32)).astype(x.dtype)


def split_cols(t, widths):
    idx, acc = [], 0
    for w in widths[:-1]:
        acc += w
        idx.append(acc)
    return jnp.split(t, idx, axis=-1)


def causal_dwconv(x, w, b):
    k = w.shape[0]
    y = lax.conv_general_dilated(x, w[:, None, :].astype(x.dtype), window_strides=(1,),
                                 padding=[(k - 1, 0)], dimension_numbers=('NWC', 'WIO', 'NWC'),
                                 feature_group_count=x.shape[-1])
    return y + b.astype(x.dtype)


def t5_bucket(dist):
    max_exact = REL_BUCKETS // 2
    d = jnp.maximum(dist, 0)
    big = max_exact + (jnp.log(jnp.maximum(d, max_exact).astype(jnp.float32) / max_exact)
                       / math.log(REL_MAX_DIST / max_exact) * (REL_BUCKETS - max_exact)).astype(jnp.int32)
    return jnp.where(d < max_exact, d, jnp.minimum(big, REL_BUCKETS - 1))


def ssd_chunked(x, dt, a, bmat, cmat):
    bsz, lp = x.shape[:2]
    nc = lp // BLOCK
    f32 = jnp.float32
    xc = x.reshape(bsz, nc, BLOCK, SSD_GROUPS, SSD_HPG, SSD_HEADDIM).astype(f32)
    dtc = dt.reshape(bsz, nc, BLOCK, SSD_GROUPS, SSD_HPG)
    bc = bmat.reshape(bsz, nc, BLOCK, SSD_GROUPS, SSD_STATE).astype(f32)
    cc = cmat.reshape(bsz, nc, BLOCK, SSD_GROUPS, SSD_STATE).astype(f32)
    cs = jnp.cumsum(dtc * a.reshape(SSD_GROUPS, SSD_HPG), axis=2)
    xdt = xc * dtc[..., None]
    cs_t = jnp.moveaxis(cs, 2, -1)
    causal = jnp.tril(jnp.ones((BLOCK, BLOCK), dtype=bool))
    lmat = jnp.exp(jnp.where(causal, cs_t[..., :, None] - cs_t[..., None, :], -jnp.inf))
    cb = jnp.einsum('bclgn,bcsgn->bcgls', cc, bc)
    y_diag = jnp.einsum('bcgls,bcgrls,bcsgrp->bclgrp', cb, lmat, xdt)
    decay_states = jnp.exp(cs[:, :, -1:] - cs)
    states = jnp.einsum('bclgn,bclgr,bclgrp->bcgrpn', bc, decay_states, xdt)
    chunk_decay = jnp.exp(cs[:, :, -1])

    def step(hst, inp):
        st, dec = inp
        return dec[..., None, None] * hst + st, hst

    h0 = jnp.zeros((bsz, SSD_GROUPS, SSD_HPG, SSD_HEADDIM, SSD_STATE), f32)
    _, prev = lax.scan(step, h0, (jnp.moveaxis(states, 1, 0), jnp.moveaxis(chunk_decay, 1, 0)))
    prev = jnp.moveaxis(prev, 0, 1)
    y_off = jnp.einsum('bclgn,bcgrpn,bclgr->bclgrp', cc, prev, jnp.exp(cs))
    return (y_diag + y_off).reshape(bsz, lp, SSD_HEADS, SSD_HEADDIM)


def gated_group_rmsnorm(y, z, w):
    g = y * jax.nn.silu(z.astype(jnp.float32))
    gg = g.reshape(g.shape[:-1] + (SSD_GROUPS, SSD_INNER // SSD_GROUPS))
    gg = gg * lax.rsqrt(jnp.mean(gg * gg, axis=-1, keepdims=True) + EPS)
    return gg.reshape(g.shape) * w.astype(jnp.float32)


def swa_sink_attention(q, k, v, rel_bias, sinks):
    bsz, lp = q.shape[:2]
    nb = lp // BLOCK
    f32 = jnp.float32
    scale = ATT_HEADDIM ** -0.5
    qb = q.reshape(bsz, nb, BLOCK, ATT_KV_HEADS, ATT_GQ, ATT_HEADDIM)
    kb = k.reshape(bsz, nb, BLOCK, ATT_KV_HEADS, ATT_HEADDIM)
    vb = v.reshape(bsz, nb, BLOCK, ATT_KV_HEADS, ATT_HEADDIM)

    def with_prev(t):
        prev = jnp.pad(t, ((0, 0), (1, 0), (0, 0), (0, 0), (0, 0)))[:, :-1]
        return jnp.concatenate([prev, t], axis=2)

    kband, vband = with_prev(kb), with_prev(vb)
    pos = jnp.arange(lp, dtype=jnp.int32).reshape(nb, BLOCK)
    k_pos = jnp.concatenate([pos - BLOCK, pos], axis=1)
    dist = pos[:, :, None] - k_pos[:, None, :]
    mask_band = (k_pos[:, None, :] >= BLOCK) & (dist >= 0) & (dist < WINDOW)
    bias_band = rel_bias[t5_bucket(dist[0])].astype(f32)
    bias_band = jnp.transpose(bias_band, (2, 0, 1)).reshape(ATT_KV_HEADS, ATT_GQ, BLOCK, 2 * BLOCK)
    s_band = jnp.einsum('bnqkgd,bnskd->bnkgqs', qb, kband, preferred_element_type=f32) * scale + bias_band
    s_band = jnp.where(mask_band[None, :, None, None], s_band, NEG)
    meta_pos = PAD + jnp.arange(N_META, dtype=jnp.int32)
    k_meta, v_meta = k[:, PAD:BLOCK], v[:, PAD:BLOCK]
    dist_m = pos[:, :, None] - meta_pos[None, None, :]
    bias_m = rel_bias[t5_bucket(dist_m)].astype(f32)
    bias_m = jnp.transpose(bias_m, (0, 3, 1, 2)).reshape(nb, ATT_KV_HEADS, ATT_GQ, BLOCK, N_META)
    s_meta = jnp.einsum('bnqkgd,bmkd->bnkgqm', qb, k_meta, preferred_element_type=f32) * scale + bias_m[None]
    s_meta = jnp.where((dist_m >= 0)[None, :, None, None], s_meta, NEG)
    sink = jnp.broadcast_to(sinks.astype(f32).reshape(1, 1, ATT_KV_HEADS, ATT_GQ, 1, 1),
                            s_meta.shape[:-1] + (1,))
    p = jax.nn.softmax(jnp.concatenate([s_meta, s_band, sink], axis=-1), axis=-1)
    p_meta = p[..., :N_META].astype(v.dtype)
    p_band = p[..., N_META:-1].astype(v.dtype)
    out = (jnp.einsum('bnkgqm,bmkd->bnqkgd', p_meta, v_meta)
           + jnp.einsum('bnkgqs,bnskd->bnqkgd', p_band, vband))
    return out.reshape(bsz, lp, ATT_Q)


def mixer_block(h, valid_f32, norm_w, w_in, conv_w, conv_b, dt_bias, a_log, d_skip, ssd_norm_w,
                w_ssd_branch, w_attn_branch, sinks, rel_bias, gate_b, w_out):
    bsz, lp = h.shape[:2]
    f32 = jnp.float32
    u = rmsnorm(h, norm_w)
    z, xbc, dt_raw, q, k, v, g_ssd, g_att = split_cols(u @ w_in.astype(u.dtype), IN_WIDTHS)
    xbc = jax.nn.silu(causal_dwconv(xbc, conv_w, conv_b))
    xs, bm, cm = split_cols(xbc, (SSD_INNER, SSD_GROUPS * SSD_STATE, SSD_GROUPS * SSD_STATE))
    dt = jax.nn.softplus(dt_raw.astype(f32) + dt_bias.astype(f32)) * valid_f32[None, :, None]
    a = -jnp.exp(a_log.astype(f32))
    xh = xs.reshape(bsz, lp, SSD_HEADS, SSD_HEADDIM)
    y = ssd_chunked(xh, dt, a, bm.reshape(bsz, lp, SSD_GROUPS, SSD_STATE),
                    cm.reshape(bsz, lp, SSD_GROUPS, SSD_STATE))
    y = y + d_skip.astype(f32)[:, None] * xh.astype(f32)
    y = gated_group_rmsnorm(y.reshape(bsz, lp, SSD_INNER), z, ssd_norm_w).astype(h.dtype)
    y_ssd = y @ w_ssd_branch.astype(h.dtype)
    att = swa_sink_attention(q.reshape(bsz, lp, ATT_HEADS, ATT_HEADDIM),
                             k.reshape(bsz, lp, ATT_KV_HEADS, ATT_HEADDIM),
                             v.reshape(bsz, lp, ATT_KV_HEADS, ATT_HEADDIM), rel_bias, sinks)
    y_att = att @ w_attn_branch.astype(h.dtype)
    gb = gate_b.astype(h.dtype)
    merged = jax.nn.sigmoid(g_ssd + gb[0]) * y_ssd + jax.nn.sigmoid(g_att + gb[1]) * y_att
    return merged @ w_out.astype(h.dtype)


def conv_ffn(h, norm_w, w_in, conv_w, conv_b, w_down):
    u = rmsnorm(h, norm_w)
    hid = causal_dwconv(u @ w_in.astype(u.dtype), conv_w, conv_b)
    up, gate = jnp.split(hid, 2, axis=-1)
    return (jax.nn.silu(gate) * up) @ w_down.astype(h.dtype)


def setup_inputs(seed: int = 0) -> dict:
    key = jax.random.key(seed)
    ks = jax.random.split(key, 24)
    f32 = jnp.float32
    nrm = lambda k, shape, s: jax.random.normal(k, shape, f32) * s
    dt0 = jnp.exp(jax.random.uniform(ks[5], (DEPTH, SSD_HEADS), f32, math.log(1e-3), math.log(1e-1)))
    return {
        'x': nrm(ks[0], (BATCH, SEQ, D_MODEL), 1.0),
        'meta_tokens': nrm(ks[1], (N_META, D_MODEL), 1.0),
        'norm_mix_w': 1.0 + nrm(ks[2], (DEPTH, D_MODEL), 0.05),
        'w_in': nrm(ks[3], (DEPTH, D_MODEL, D_IN_PROJ), D_MODEL ** -0.5),
        'ssd_conv_w': nrm(ks[4], (DEPTH, SSD_CONV, SSD_XBC), SSD_CONV ** -0.5),
        'ssd_conv_b': nrm(ks[6], (DEPTH, SSD_XBC), 0.01),
        'ssd_dt_bias': dt0 + jnp.log(-jnp.expm1(-dt0)),
        'ssd_a_log': jnp.log(jax.random.uniform(ks[7], (DEPTH, SSD_HEADS), f32, 1.0, 16.0)),
        'ssd_d': 1.0 + nrm(ks[8], (DEPTH, SSD_HEADS), 0.1),
        'ssd_norm_w': 1.0 + nrm(ks[9], (DEPTH, SSD_INNER), 0.05),
        'w_ssd_branch': nrm(ks[10], (DEPTH, SSD_INNER, D_MODEL), SSD_INNER ** -0.5),
        'w_attn_branch': nrm(ks[11], (DEPTH, ATT_Q, D_MODEL), ATT_Q ** -0.5),
        'attn_sinks': nrm(ks[12], (DEPTH, ATT_HEADS), 0.5),
        'rel_bias': nrm(ks[13], (REL_BUCKETS, ATT_HEADS), 0.5),
        'gate_b': nrm(ks[14], (DEPTH, N_BRANCH, D_MODEL), 0.1),
        'w_out': nrm(ks[15], (DEPTH, D_MODEL, D_MODEL), D_MODEL ** -0.5),
        'norm_ffn_w': 1.0 + nrm(ks[16], (DEPTH, D_MODEL), 0.05),
        'w_ffn_in': nrm(ks[17], (DEPTH, D_MODEL, 2 * D_FF), D_MODEL ** -0.5),
        'ffn_conv_w': nrm(ks[18], (DEPTH, FFN_CONV, 2 * D_FF), FFN_CONV ** -0.5),
        'ffn_conv_b': nrm(ks[19], (DEPTH, 2 * D_FF), 0.01),
        'w_ffn_out': nrm(ks[20], (DEPTH, D_FF, D_MODEL), D_FF ** -0.5),
        'norm_final_w': 1.0 + nrm(ks[21], (D_MODEL,), 0.05),
    }


def reference(x, meta_tokens, norm_mix_w, w_in, ssd_conv_w, ssd_conv_b, ssd_dt_bias, ssd_a_log, ssd_d,
              ssd_norm_w, w_ssd_branch, w_attn_branch, attn_sinks, rel_bias, gate_b, w_out, norm_ffn_w,
              w_ffn_in, ffn_conv_w, ffn_conv_b, w_ffn_out, norm_final_w):
    bsz, seq = x.shape[:2]
    lp = seq + BLOCK
    meta = jnp.broadcast_to(meta_tokens.astype(x.dtype)[None], (bsz, N_META, D_MODEL))
    h = jnp.concatenate([jnp.zeros((bsz, PAD, D_MODEL), x.dtype), meta, x], axis=1)
    valid = jnp.arange(lp) >= PAD
    valid_f32 = valid.astype(jnp.float32)
    valid_h = valid.astype(x.dtype)[None, :, None]
    for i in range(DEPTH):
        h = h + valid_h * mixer_block(h, valid_f32, norm_mix_w[i], w_in[i], ssd_conv_w[i], ssd_conv_b[i],
                                      ssd_dt_bias[i], ssd_a_log[i], ssd_d[i], ssd_norm_w[i], w_ssd_branch[i],
                                      w_attn_branch[i], attn_sinks[i], rel_bias, gate_b[i], w_out[i])
        h = h + valid_h * conv_ffn(h, norm_ffn_w[i], w_ffn_in[i], ffn_conv_w[i], ffn_conv_b[i], w_ffn_out[i])
    y = rmsnorm(h, norm_final_w)
    return y[:, BLOCK:]
```

```python
import math
from contextlib import ExitStack
import numpy as np
import concourse.bass as bass
import concourse.mybir as mybir
from concourse.bass_utils import run_bass_kernel_spmd

F32 = mybir.dt.float32
BF16 = mybir.dt.bfloat16
AF = mybir.ActivationFunctionType
ALU = mybir.AluOpType
AX = mybir.AxisListType

G = 1
DEBUG = {}
DEBUG_ON = False
DEBUG_J = 2
import os
STAGE = int(os.environ.get("K_STAGE", "99"))
SUB = int(os.environ.get("K_SUB", "99"))
MAXOPS = int(os.environ.get("K_OPS", "100000000"))
SUBG = int(os.environ.get("K_SUBG", "0"))
SERIAL = int(os.environ.get("K_SERIAL", "1"))
EPS = 1e-6
NBLK = 18
EPOCH = 3000
CV = {}
_o = 0
for _n, _w in [("nfin", 1024), ("dtb", 32), ("alog", 32), ("dskip", 32), ("sinks", 16), ("convw", 96),
               ("convb", 24), ("fconvw", 132), ("fconvb", 44), ("gateb", 16), ("vflag", 18), ("valid", 18),
               ("sel", 64), ("maskj", 8), ("m16", 1)]:
    CV[_n] = (_o, _w)
    _o += _w
NV = _o


class Prog:
    def __init__(self):
        self.ops = []
        self.lw = {}
        self.rd = {}
        self.bulk = set()

    def mark(self, n_):
        if STAGE <= n_:
            self.stopped = True

    def sub(self, n_):
        if SUB <= n_ and getattr(self, "cur_gi", 0) >= SUBG:
            self.stopped = True

    def op(self, eng, fn, r=(), w=(), kind="c", key=None):
        if len(self.ops) >= MAXOPS and not getattr(self, "final", False):
            self.stopped = True
        if getattr(self, "stopped", False):
            return -1
        i = len(self.ops)
        deps = set()
        for x in r:
            if x in self.lw:
                deps.add(self.lw[x])
        for x in w:
            if x in self.lw:
                deps.add(self.lw[x])
            deps.update(self.rd.get(x, ()))
        for x in r:
            self.rd.setdefault(x, []).append(i)
        for x in w:
            self.lw[x] = i
            self.rd[x] = []
        if SERIAL and kind == "c" and fn is not None:
            if getattr(self, "last_c", None) is not None:
                deps.add(self.last_c)
            self.last_c = i
        deps.discard(i)
        self.ops.append(dict(eng=eng, fn=fn, deps=deps, kind=kind, key=key))
        return i

    def emit(self, nc, es, block):
        ops = self.ops
        engs = ["pe", "act", "dve", "pool", "sp"]
        need = [False] * len(ops)
        for o in ops:
            latest = {}
            for d in o["deps"]:
                D = ops[d]
                if D["kind"] == "c":
                    if D["eng"] == "pe" and o["eng"] == "pe":
                        continue
                    if latest.get(D["eng"], -1) < d:
                        latest[D["eng"]] = d
            for d in latest.values():
                need[d] = True
            o["deps"] = set(d for d in o["deps"] if ops[d]["kind"] != "c" or latest.get(ops[d]["eng"]) == d)
        cnt = {e: 0 for e in engs}
        keycnt = {}
        for i, o in enumerate(ops):
            if o["kind"] == "c":
                if need[i]:
                    cnt[o["eng"]] += 1
                    o["sig"] = cnt[o["eng"]]
            elif o["kind"] == "dma":
                keycnt[o["key"]] = keycnt.get(o["key"], 0) + 1
                o["ord"] = keycnt[o["key"]]
        sems = {}
        for e in engs:
            n = (cnt[e] + EPOCH - 1) // EPOCH + 1
            sems[e] = [es.enter_context(nc.semaphore(f"s_{e}_{q}")) for q in range(n)]
        ksem = {k: es.enter_context(nc.semaphore(f"k_{k}")) for k in keycnt}
        ccsem = es.enter_context(nc.semaphore("ccsem"))
        per = {e: [] for e in engs}
        for i, o in enumerate(ops):
            per[o["eng"]].append(i)

        def run(eng_name, eng):
            waited = {}
            for i in per[eng_name]:
                o = ops[i]
                wants = {}
                for d in o["deps"]:
                    D = ops[d]
                    if D["kind"] == "c":
                        if D["eng"] == "pe" and eng_name == "pe":
                            continue
                        n = D["sig"]
                        k = ("e", D["eng"], (n - 1) // EPOCH)
                        v = (n - 1) % EPOCH + 1
                    elif D["kind"] == "dma":
                        if D["key"] in self.bulk and o["kind"] == "dma" and o["key"] == D["key"]:
                            continue
                        k = ("k", D["key"])
                        v = 16 * (keycnt[D["key"]] if D["key"] in self.bulk else D["ord"])
                    elif D["kind"] == "cc":
                        k = ("cc",)
                        v = 1
                    else:
                        continue
                    if wants.get(k, 0) < v:
                        wants[k] = v
                for k, v in wants.items():
                    if k[0] == "e":
                        done = waited.get(("e", k[1]), (-1, 0))
                        if (k[2], v) <= done:
                            continue
                        waited[("e", k[1])] = (k[2], v)
                        eng.wait_ge(sems[k[1]][k[2]], v)
                    elif k[0] == "k":
                        if waited.get(k, 0) >= v:
                            continue
                        waited[k] = v
                        eng.wait_ge(ksem[k[1]], v)
                    else:
                        if waited.get(k, 0) >= v:
                            continue
                        waited[k] = v
                        eng.wait_ge(ccsem, v)
                if o["fn"] is None:
                    for kk, nn in keycnt.items():
                        if waited.get(("k", kk), 0) < 16 * nn:
                            waited[("k", kk)] = 16 * nn
                            eng.wait_ge(ksem[kk], 16 * nn)
                    continue
                inst = o["fn"](eng)
                if o["kind"] == "dma":
                    inst.then_inc(ksem[o["key"]], 16)
                elif o["kind"] == "cc":
                    inst.then_inc(ccsem)
                elif "sig" in o:
                    n = o["sig"]
                    inst.then_inc(sems[eng_name][(n - 1) // EPOCH], 1)

        @block.tensor
        def _(e):
            run("pe", e)

        @block.scalar
        def _(e):
            run("act", e)

        @block.vector
        def _(e):
            run("dve", e)

        @block.gpsimd
        def _(e):
            run("pool", e)

        @block.sync
        def _(e):
            run("sp", e)


def build_nc():
    nc = bass.Bass("TRN2", target_bir_lowering=False)
    T = G * 128
    P = Prog()
    es = ExitStack()

    def din(name, shape):
        return nc.dram_tensor(name, shape, F32, kind="ExternalInput").ap()

    xin = din("xin", [NBLK, 128, 1024])
    meta = din("meta", [128, 1024])
    w_in = din("w_in", [1024, 8480])
    w_ssd = din("w_ssd", [2048, 1024])
    w_att = din("w_att", [1024, 1024])
    w_out = din("w_out", [1024, 1024])
    w_fi = din("w_fi", [1024, 5632])
    w_fo = din("w_fo", [2816, 1024])
    cst = din("cst", [128, 512])
    cnorm = din("cnorm", [128, 4096])
    cvec = din("cvec", [128, NV])
    bandb = din("bandb", [128, 4096])
    bandm = din("bandm", [128, 4096])
    metab = din("metab", [128, 6144])
    metam = din("metam", [128, 6144])
    yout = nc.dram_tensor("y", [16, 128, 1024], F32, kind="ExternalOutput").ap()

    WIN2 = nc.dram_tensor("WIN2", [1024, 8704], BF16).ap()
    WSSD = nc.dram_tensor("WSSD", [2048, 1024], BF16).ap()
    WATT = nc.dram_tensor("WATT", [1024, 1024], BF16).ap()
    WOUT = nc.dram_tensor("WOUT", [1024, 1024], BF16).ap()
    WFI2 = nc.dram_tensor("WFI2", [1024, 5632], BF16).ap()
    WFO = nc.dram_tensor("WFO", [2816, 1024], BF16).ap()
    cc_in = nc.dram_tensor("cc_in", [129, 2048], F32)
    cc_out = nc.dram_tensor("cc_out", [8 * 129, 2048], F32)

    def sb(name, shape, dt=F32):
        return es.enter_context(nc.sbuf_tensor(name, shape, dt))

    NPS = 8
    psums = [es.enter_context(nc.psum_tensor(f"ps{i}", [128, 512], F32)) for i in range(NPS)]
    psi = [0]

    def PS():
        i = psi[0] % 5
        psi[0] += 1
        return psums[i], ("ps", i)

    cst_t = sb("cst_t", [128, 512])
    ident = sb("ident", [128, 128], BF16)
    tri = sb("tri", [128, 128], BF16)
    ones = sb("ones", [128, 128], BF16)
    cv = sb("cv", [128, NV])
    cnb = sb("cnb", [128, 4096], BF16)
    small = sb("small", [128, 512])
    ebias = sb("ebias", [128, 4096], BF16)
    emeta = sb("emeta", [128, 6144], BF16)
    hbuf = [sb(f"hbuf{i}", [128, max(G, 2), 1024]) for i in range(2)]
    ubf = sb("ubf", [128, 1024], BF16)
    uT = sb("uT", [128, 8, max(T, 256)], BF16)
    zs = sb("zs", [128, G, 2048], BF16)
    xs_tok = sb("xs_tok", [128, G, 2048], BF16)
    B_tok = sb("B_tok", [128, G, 512], BF16)
    BT = sb("BT", [128, 4, T], BF16)
    CT = sb("CT", [128, 4, T], BF16)
    qT = sb("qT", [128, 2, 8, T], BF16)
    kT = sb("kT", [128, 2, 128 + T], BF16)
    kmT = sb("kmT", [128, 2, 128], BF16)
    vaug = sb("vaug", [128, G + 1, 2, 65], BF16)
    vmeta = sb("vmeta", [128, 2, 65], BF16)
    dts = sb("dts", [128, 16, 32])
    dtb16 = sb("dtb16", [128, 4, 32], BF16)
    totacc = sb("totacc", [128, 32])
    xdt = sb("xdt", [128, 2048], BF16)
    xdtd = sb("xdtd", [128, 2048], BF16)
    NA = 3
    Abuf = [sb(f"A{i}", [128, 4, 128], BF16) for i in range(NA)]
    Dd = [sb(f"Dd{i}", [128, 4, 128], BF16) for i in range(2)]
    Lx = [sb(f"Lx{i}", [128, 512], BF16) for i in range(2)]
    MT = [sb(f"MT{i}", [128, 4, 128], BF16) for i in range(2)]
    CBm = sb("CBm", [128, 4, 128], BF16)
    t_sb = sb("t_sb", [128, 2048], BF16)
    gg = [sb(f"gg{i}", [128, 512]) for i in range(2)]
    gn = sb("gn", [128, 2048], BF16)
    gnT = sb("gnT", [128, 16, T], BF16)
    attT = sb("attT", [128, 8, T], BF16)
    att_tok = sb("att_tok", [128, 1024], BF16)
    Pexp = [sb(f"Pexp{i}", [128, 512], BF16) for i in range(2)]
    PT = [sb(f"PT{i}", [128, 512], BF16) for i in range(2)]
    Pme = sb("Pme", [128, 512], BF16)
    PTm = sb("PTm", [128, 512], BF16)
    den = sb("den", [128, 32])
    S = sb("S", [128, 2048])
    S_bf = sb("S_bf", [128, 2048], BF16)
    tgT = sb("tgT", [128, 16, T], BF16)
    mT = sb("mT", [128, 8, T], BF16)
    m1 = [sb(f"m1_{i}", [128, T]) for i in range(2)]
    actT = sb("actT", [128, 22, T], BF16)
    NX = 3
    xpre = [sb(f"xpre{i}", [128, 3 + max(T, 256)], BF16) for i in range(NX)]
    hpre = [sb(f"hpre{i}", [128, 2 + T], BF16) for i in range(NX)]
    dg = [sb(f"dg{i}", [128, 5, 128], BF16) for i in range(2)]
    halo = sb("halo", [128, 24, 3], BF16)
    halo0 = sb("halo0", [128, 24, 3], BF16)
    fhalo = sb("fhalo", [128, 44, 2], BF16)
    NSL = 6
    slabs = [sb(f"slab{i}", [128, 8, 512], BF16) for i in range(NSL)]
    LD = sb("LD", [128, 8, 32])
    prodt = sb("prodt", [128, 8, 32])
    coef = sb("coef", [128, 8, 32])
    rs = sb("rs", [128, 8])

    def cvs(name, a=0, b=None):
        o, w = CV[name]
        if b is None:
            b = w
        return cv[:, o + a:o + b]

    a_b = small[:, 0:32]
    esink = small[:, 32:48]
    hgb = small[:, 48:64]
    vh = small[:, 64:82]

    def dma(q, out, in_, r, w, key):
        P.op(q, lambda e, o=out, i=in_: e.dma_start(out=o, in_=i), r, w, kind="dma", key=key)

    def mm(out, lhsT, rhs, start, stop, r, w):
        P.op("pe", lambda e, o=out, l=lhsT, rh=rhs, s=start, t=stop: e.matmul(o, l, rh, start=s, stop=t), r, w)

    def tr(out, in_, r, w):
        P.op("pe", lambda e, o=out, i=in_: e.transpose(o, i, ident[:]), tuple(r) + ("ident",), w)

    def act(out, in_, func, r, w, bias=None, scale=None, accum=None):
        def f(e, o=out, i=in_, fu=func, b=bias, s=scale, a=accum):
            kw = {}
            if b is not None:
                kw["bias"] = b
            if s is not None:
                kw["scale"] = s
            if a is not None:
                kw["accum_out"] = a
            return e.activation(o, i, fu, **kw)
        P.op("act", f, r, w)

    def ts(eng, out, in0, s1, s2, op0, op1, r, w):
        if op1 is None:
            P.op(eng, lambda e, o=out, i=in0, a=s1, p0=op0: e.tensor_scalar(o, i, a, None, p0), r, w)
        else:
            P.op(eng, lambda e, o=out, i=in0, a=s1, b=s2, p0=op0, p1=op1: e.tensor_scalar(o, i, a, b, p0, p1), r, w)

    def tt(eng, out, in0, in1, op, r, w):
        P.op(eng, lambda e, o=out, a=in0, b=in1, p=op: e.tensor_tensor(o, a, b, p), r, w)

    def stt(out, in0, sc, in1, op0, op1, r, w):
        P.op("dve", lambda e, o=out, a=in0, s=sc, b=in1, p0=op0, p1=op1: e.scalar_tensor_tensor(o, a, s, b, p0, p1), r, w)

    def cp(eng, out, in_, r, w):
        if eng == "act":
            P.op("act", lambda e, o=out, i=in_: e.copy(o, i), r, w)
        else:
            P.op(eng, lambda e, o=out, i=in_: e.tensor_copy(o, i), r, w)

    def dbg(name, ap, res, j=None):
        if not DEBUG_ON or (j is not None and j != DEBUG_J):
            return
        shp = list(ap.shape)
        t_ = nc.dram_tensor("dbg_" + name, shp, ap.dtype, kind="ExternalOutput").ap()
        DEBUG[name] = (shp, ap.dtype)
        dma("sp", t_, ap, tuple(res), (), "dbg_" + name)

    slab_i = [0]
    qsel = [0]

    def load_slab(src, kc0, kn, c0, ncol, srcres):
        i = slab_i[0] % NSL
        slab_i[0] += 1
        q = "sp" if (qsel[0] % 2 == 0) else "act"
        qsel[0] += 1
        q = "sp"
        ap_in = src[kc0 * 128:(kc0 + kn) * 128, c0:c0 + ncol].rearrange("(c p) n -> p c n", p=128)
        dma(q, slabs[i][:, 0:kn, 0:ncol], ap_in, (srcres,), (("slab", i),), f"slab{i}")
        return slabs[i], ("slab", i)

    class Stream:
        def __init__(self, descs, depth):
            self.descs = descs
            self.depth = depth
            self.loaded = []
            self.pos = 0

        def _ensure(self, upto):
            while len(self.loaded) <= upto and len(self.loaded) < len(self.descs):
                self.loaded.append(load_slab(*self.descs[len(self.loaded)]))

        def next(self, desc):
            assert self.descs[self.pos] == desc, (self.pos, self.descs[self.pos], desc)
            self._ensure(self.pos + self.depth)
            r = self.loaded[self.pos]
            self.pos += 1
            return r

    def d_win(si):
        return (WIN2, 0, 8, si * 512, 416 if si == 16 else 512, "scr_win%d" % (0 if (4 <= si < 10 or si == 16) else 1))

    def d_ssd(kh, og):
        return (WSSD, kh * 8, 8, og * 512, 512, "scr_ssd")

    def d_att(og):
        return (WATT, 0, 8, og * 512, 512, "scr_att")

    def d_out(hf):
        return (WOUT, 0, 8, hf * 512, 512, "scr_out")

    def d_fi(si):
        return (WFI2, 0, 8, si * 512, 512, "scr_fi")

    def d_fo(kh, hf):
        return (WFO, kh * 8, 8 if kh < 2 else 6, hf * 512, 512, "scr_fo")

    def sched_prologue():
        return [d_win(s) for s in range(4, 10)] + [d_win(16)]

    def sched_p1():
        return [d_win(s) for s in range(4, 9)] + [d_win(16)]

    def sched_p2(full):
        L = [d_win(s) for s in range(4, 10)] + [d_win(10), d_win(11), d_win(16)] + [d_win(s) for s in range(0, 4)]
        L += [d_win(s) for s in range(12, 16)]
        for og in range(2):
            L += [d_ssd(0, og), d_ssd(1, og), d_att(og)]
        L += [d_out(0), d_out(1)]
        L += [d_fi(s) for s in range(11)]
        if full:
            for hf in range(2):
                L += [d_fo(0, hf), d_fo(1, hf), d_fo(2, hf)]
        return L

    p2groups = [[1]] + [list(range(2 + g * G, 2 + (g + 1) * G)) for g in range(16 // G)]
    p1groups = [list(range(1 + g * G, 1 + (g + 1) * G)) for g in range(16 // G)]
    descs = sched_prologue()
    for _ in p1groups:
        descs += sched_p1()
    for gi, js in enumerate(p2groups):
        descs += sched_p2(gi > 0)
    W = Stream(descs, NSL - 3)

    HR0 = (("hbuf0", 0), ("hbuf0", 1))
    HR1 = (("hbuf1", 0), ("hbuf1", 1))
    dma("sp", cst_t[:], cst, (), ("cst_t",), "c0")
    dma("sp", cv[:], cvec, (), ("cv",), "c1")
    cp("dve", ident[:], cst_t[:, 0:128], ("cst_t",), ("ident",))
    cp("dve", tri[:], cst_t[:, 128:256], ("cst_t",), ("tri",))
    cp("dve", ones[:], cst_t[:, 384:512], ("cst_t",), ("ones",))
    su = cst_t[:, 256:384]
    for hfi in range(2):
        dma("sp", hbuf[0][:, 0:2, :], cnorm[:, hfi * 2048:(hfi + 1) * 2048].rearrange("p (a b) -> p a b", a=2),
            (), HR0, "c2")
        cp("dve", cnb[:, hfi * 2048:(hfi + 1) * 2048].rearrange("p (a b) -> p a b", a=2), hbuf[0][:, 0:2, :],
           HR0, ("cnb",))
    nmw_b = cnb[:, 0:1024]
    nfw_b = cnb[:, 1024:2048]
    ssdnw_b = cnb[:, 2048:4096]
    for hfi in range(2):
        sl = slice(hfi * 2048, (hfi + 1) * 2048)
        v2 = lambda t_: t_.rearrange("p (a b) -> p a b", a=2)
        dma("sp", hbuf[0][:, 0:2, :], v2(bandb[:, sl]), (), HR0, "c2")
        dma("sp", hbuf[1][:, 0:2, :], v2(bandm[:, sl]), (), HR1, "c3")
        act(hbuf[0][:, 0:2, :], hbuf[0][:, 0:2, :], AF.Exp, HR0, HR0)
        tt("dve", v2(ebias[:, sl]), hbuf[0][:, 0:2, :], hbuf[1][:, 0:2, :], ALU.mult, HR0 + HR1, ("ebias",))
    for hfi in range(3):
        sl = slice(hfi * 2048, (hfi + 1) * 2048)
        v2 = lambda t_: t_.rearrange("p (a b) -> p a b", a=2)
        dma("sp", hbuf[0][:, 0:2, :], v2(metab[:, sl]), (), HR0, "c2")
        dma("sp", hbuf[1][:, 0:2, :], v2(metam[:, sl]), (), HR1, "c3")
        act(hbuf[0][:, 0:2, :], hbuf[0][:, 0:2, :], AF.Exp, HR0, HR0)
        tt("dve", v2(emeta[:, sl]), hbuf[0][:, 0:2, :], hbuf[1][:, 0:2, :], ALU.mult, HR0 + HR1,
           ("emeta",))
    act(a_b, cvs("alog"), AF.Exp, ("cv",), ("small_a",))
    ts("dve", a_b, a_b, -1.0, None, ALU.mult, None, ("small_a",), ("small_a",))
    act(esink, cvs("sinks"), AF.Exp, ("cv",), ("small_b",))
    ts("dve", hgb, cvs("gateb"), 0.5, None, ALU.mult, None, ("cv",), ("small_c",))
    ts("dve", vh, cvs("valid"), 0.5, None, ALU.mult, None, ("cv",), ("small_d",))
    P.op("dve", lambda e: e.memset(totacc[:], 0.0), (), ("totacc",))
    P.op("dve", lambda e: e.memset(dts[:], 0.0), (), tuple(("dts", q) for q in range(10)) + tuple(("dtr", q) for q in range(G)))
    P.op("dve", lambda e: e.memset(S[:], 0.0), (), ("S",))
    P.op("dve", lambda e: e.memset(fhalo[:], 0.0), (), ("fhalo",))
    P.op("dve", lambda e: e.memset(vmeta[:], 0.0), (), ("vmeta",))
    P.op("dve", lambda e: e.memset(qT[:], 0.0), (), ("qT",))

    def prep(dst, src, rows, dc0, sc0, ncol, res):
        P.bulk.add(res)
        for rc in range(rows // 128):
            dma("pool", dst[rc * 128:(rc + 1) * 128, dc0:dc0 + ncol], src[rc * 128:(rc + 1) * 128, sc0:sc0 + ncol],
                (), (res,), res)

    prep(WIN2, w_in, 1024, 2048, 2048, 2048, "scr_win0")
    prep(WIN2, w_in, 1024, 4096, 4096, 1024, "scr_win0")
    for kv in range(2):
        for dup in range(2):
            prep(WIN2, w_in, 1024, 8192 + kv * 128 + dup * 64, 6176 + kv * 64, 64, "scr_win0")
    prep(WIN2, w_in, 1024, 8448, 6304, 128, "scr_win0")
    prep(WIN2, w_in, 1024, 8576, 5120, 32, "scr_win0")
    prep(WIN2, w_in, 1024, 0, 0, 2048, "scr_win1")
    prep(WIN2, w_in, 1024, 5120, 5152, 1024, "scr_win1")
    prep(WIN2, w_in, 1024, 6144, 6432, 2048, "scr_win1")
    prep(WSSD, w_ssd, 2048, 0, 0, 1024, "scr_ssd")
    prep(WATT, w_att, 1024, 0, 0, 1024, "scr_att")
    prep(WOUT, w_out, 1024, 0, 0, 1024, "scr_out")
    prep(WFI2, w_fi, 1024, 0, 2816, 2816, "scr_fi")
    prep(WFI2, w_fi, 1024, 2816, 0, 2816, "scr_fi")
    prep(WFO, w_fo, 2816, 0, 0, 1024, "scr_fo")

    rsq = [0]

    def rstd_chain(ssap, n_inv, rr, wr):
        ts("dve", ssap, ssap, n_inv, EPS, ALU.mult, ALU.add, rr, wr)
        act(ssap, ssap, AF.Ln, wr, wr)
        act(ssap, ssap, AF.Exp, wr, wr, scale=-0.5)

    def norm_T(src, srcres, wb, dstT, dstres, col0, evac_eng):
        k = rsq[0] % 8
        rsq[0] += 1
        ss = rs[:, k:k + 1]
        rres = ("rs", k)
        act(ubf[:], src, AF.Square, (srcres,), ("ubf", rres), accum=ss)
        rstd_chain(ss, 1.0 / 1024, (rres,), (rres,))
        stt(ubf[:], src, ss, wb, ALU.mult, ALU.mult, (srcres, rres, "cnb"), ("ubf",))
        ps, pr = PS()
        pb = ps[:].bitcast(BF16)
        for c in range(8):
            tr(pb[:, c * 128:(c + 1) * 128], ubf[:, c * 128:(c + 1) * 128], ("ubf",), (pr,))
        cp(evac_eng, dstT[:, :, col0:col0 + 128], pb[:, 0:1024].rearrange("p (c t) -> p c t", c=8), (pr,), (dstres,))

    def fm_chunk(slab, sres, cc, rhsT, rres, n):
        ps, pr = PS()
        for kc in range(8):
            mm(ps[:, 0:n], slab[:, kc, cc * 128:(cc + 1) * 128], rhsT[:, kc, 0:n], kc == 0, kc == 7, (sres, rres), (pr,))
        return ps, pr

    dgi = [0]

    def gen_diag(wcols, ntap, bcol):
        i = dgi[0] % 2
        dgi[0] += 1
        d = dg[i]
        for k in range(ntap):
            ts("pool", d[:, k, :], ident[:], wcols[k], 0.0, ALU.mult, ALU.add, ("ident", "cv"), (("dg", i, k),))
        if bcol is not None:
            ts("pool", d[:, 4, :], ident[:], bcol, 0.0, ALU.mult, ALU.add, ("ident", "cv"), (("dg", i, 4),))
        return d, ("dg", i)

    xpi = [0]

    def xbc_chunk(c, slab, sres, cc, n, nblk, want_tok, want_fm):
        ps, pr = fm_chunk(slab, sres, cc, uT, "uT", n)
        i = xpi[0] % NX
        xpi[0] += 1
        xp = xpre[i]
        xr = ("xpre", i)
        cp("pool", xp[:, 0:3], halo[:, c, :], (("halo", c),), (xr,))
        cp("act", xp[:, 3:3 + n], ps[:, 0:n], (pr,), (xr,))
        cp("pool", halo[:, c, :], xp[:, n:n + 3], (xr,), (("halo", c),))
        o, _ = CV["convw"]
        d, dr = gen_diag([cv[:, o + c * 4 + k:o + c * 4 + k + 1] for k in range(4)], 4,
                         cvs("convb", c, c + 1) if want_tok else None)
        outs = {}
        if want_tok:
            dst, dstres, dcol = (xs_tok, "xs_tok", c * 128) if c < 16 else (B_tok, "B_tok", (c - 16) * 128)
            for bi in range(nblk):
                ps, pr = PS()
                o_ = ps[:, 0:128]
                for k in range(4):
                    mm(o_, xp[:, k + bi * 128:k + bi * 128 + 128], d[:, k, :], k == 0, False, (xr, dr + (k,)), (pr,))
                mm(o_, ones[:], d[:, 4, :], False, True, ("ones", dr + (4,)), (pr,))
                act(dst[:, bi, dcol:dcol + 128], o_, AF.Silu, (pr,), (dstres,))
        if want_fm:
            ps2, pr2 = PS()
            for k in range(4):
                mm(ps2[:, 0:n], d[:, k, :], xp[:, k:k + n], k == 0, k == 3, (dr + (k,), xr), (pr2,))
            outs["fm"] = (ps2, pr2)
        return outs

    def conv_tok(group, nblk, dst, dstres, dcol0):
        for bi in range(nblk):
            ps, pr = PS()
            for q4, (xp, xr, d, dr) in enumerate(group):
                o_ = ps[:, q4 * 128:(q4 + 1) * 128]
                for k in range(4):
                    mm(o_, xp[:, k + bi * 128:k + bi * 128 + 128], d[:, k, :], k == 0, False, (xr, dr + (k,)), (pr,))
                mm(o_, ones[:], d[:, 4, :], False, True, ("ones", dr + (4,)), (pr,))
            act(dst[:, bi, dcol0:dcol0 + 512], ps[:], AF.Silu, (pr,), (dstres,))

    def load_x(js, buf, bres):
        for bi, j in enumerate(js):
            dma("sp", buf[:, bi, :], xin[j], (), ((bres, bi),), f"x_{bres}_{bi}")

    def dt_chain(bi, j, phase1):
        D = lambda i: dts[:, i, :]
        R = lambda *i: tuple(("dts", q) for q in i)
        act(D(1), D(0), AF.Exp, R(0), R(1))
        act(D(1), D(1), AF.Ln, R(1), R(1), bias=1.0)
        ts("dve", D(2), D(1), cvs("valid", j, j + 1), None, ALU.mult, None, R(1) + ("cv",), R(2))
        tt("dve", D(3), D(2), a_b, ALU.mult, R(2) + ("small_a",), R(3))
        cp("dve", dtb16[:, 0, :], D(3), R(3), (("d16", 0),))
        tt("dve", dtb16[:, 1, :], D(3), dtb16[:, 0, :], ALU.subtract, R(3) + (("d16", 0),), (("d16", 1),))
        ps, pr = PS()
        mm(ps[:, 0:32], tri[:], dtb16[:, 0, :], True, False, ("tri", ("d16", 0)), (pr,))
        mm(ps[:, 0:32], tri[:], dtb16[:, 1, :], False, True, ("tri", ("d16", 1)), (pr,))
        mm(ps[:, 32:64], ones[:], dtb16[:, 0, :], True, False, ("ones", ("d16", 0)), (pr,))
        mm(ps[:, 32:64], ones[:], dtb16[:, 1, :], False, True, ("ones", ("d16", 1)), (pr,))
        cp("act", D(5), ps[:, 0:32], (pr,), R(5))
        act(D(6), ps[:, 0:32], AF.Exp, (pr,), R(6))
        act(D(7), ps[:, 32:64], AF.Exp, (pr,), R(7))
        tt("dve", D(8), ps[:, 32:64], D(5), ALU.subtract, (pr,) + R(5), R(8))
        if phase1:
            tt("dve", totacc[:], ps[:, 32:64], totacc[:], ALU.add, (pr, "totacc"), ("totacc",))
        act(D(8), D(8), AF.Exp, R(8), R(8))
        tt("dve", dtb16[:, 3, :], D(2), D(8), ALU.mult, R(2, 8), (("d16", 3),))
        cp("dve", dtb16[:, 2, :], D(2), R(2), (("d16", 2),))
        xv = xs_tok[:, bi, :].rearrange("p (h d) -> p h d", d=64)
        bc = lambda a: a.unsqueeze(2).to_broadcast([128, 32, 64])
        tt("pool", xdtd[:].rearrange("p (h d) -> p h d", d=64), xv, bc(dtb16[:, 3, :]), ALU.mult,
           ("xs_tok", ("d16", 3)), ("xdtd",))
        if not phase1:
            tt("pool", xdt[:].rearrange("p (h d) -> p h d", d=64), xv, bc(dtb16[:, 2, :]), ALU.mult,
               ("xs_tok", ("d16", 2)), ("xdt",))

    def state_update(bi, phase1):
        X = () if phase1 else ("SSER",)
        for g in range(4):
            ps, pr = PS()
            mm(ps[:], B_tok[:, bi, g * 128:(g + 1) * 128], xdtd[:, g * 512:(g + 1) * 512], True, True,
               ("B_tok", "xdtd") + X, (pr,) + X)
            Sg = S[:, g * 512:(g + 1) * 512]
            Sv = Sg.rearrange("p (h d) -> p h d", d=64)
            tt("pool", Sv, Sv, dts[:, 7, g * 8:(g + 1) * 8].unsqueeze(2).to_broadcast([128, 8, 64]), ALU.mult,
               (("S", g), ("dts", 7)) + X, (("S", g),) + X)
            tt("dve", Sg, ps[:], Sg, ALU.add, (pr, ("S", g)) + X, (("S", g),) + X)
            if not phase1:
                cp("dve", S_bf[:, g * 512:(g + 1) * 512], Sg, (("S", g),) + X, (("S_bf", g),) + X)

    ai = [0]
    li = [0]

    def ssd_block(bi, j):
        bc = slice(bi * 128, (bi + 1) * 128)
        for g in range(4):
            ps, pr = PS()
            mm(ps[:], CT[:, g, bc], S_bf[:, g * 512:(g + 1) * 512], True, True, ("CT", ("S_bf", g)), (pr,))
            tt("dve", t_sb[:, g * 512:(g + 1) * 512].rearrange("p (h d) -> p h d", d=64),
               ps[:].rearrange("p (h d) -> p h d", d=64),
               dts[:, 6, g * 8:(g + 1) * 8].unsqueeze(2).to_broadcast([128, 8, 64]), ALU.mult,
               (pr, ("dts", 6)), (("t_sb", g),))
        ps, pr = PS()
        for g in range(4):
            mm(ps[:, g * 128:(g + 1) * 128], BT[:, g, bc], CT[:, g, bc], True, True, ("BT", "CT"), (pr,))
        tt("dve", CBm[:], ps[:].rearrange("p (g l) -> p g l", g=4), tri[:].unsqueeze(1).to_broadcast([128, 4, 128]),
           ALU.mult, (pr, "tri"), ("CBm",))
        ybanks = [None] * 4
        for hg in range(8):
            g = hg // 2
            a_i = ai[0] % NA
            ai[0] += 1
            A = Abuf[a_i]
            Dm = Dd[hg % 2]
            o_, _ = CV["dskip"]
            for hh in range(4):
                h = hg * 4 + hh
                ts("pool", A[:, hh, :], su, dts[:, 3, h:h + 1], 0.0, ALU.mult, ALU.add, ("cst_t", ("dts", 3)),
                   (("A", a_i, hh),))
                ts("pool", Dm[:, hh, :], ident[:], cv[:, o_ + h:o_ + h + 1], 0.0, ALU.mult, ALU.add, ("ident", "cv"),
                   (("Dd", hg % 2, hh),))
            ps, pr = PS()
            for hh in range(4):
                mm(ps[:, hh * 128:(hh + 1) * 128], A[:, hh, :], tri[:], True, True, (("A", a_i, hh), "tri"), (pr,))
            l_i = li[0] % 2
            li[0] += 1
            act(Lx[l_i][:], ps[:], AF.Exp, (pr,), (("Lx", l_i),))
            tt("dve", MT[l_i][:], Lx[l_i][:].rearrange("p (a l) -> p a l", a=4),
               CBm[:, g, :].unsqueeze(1).to_broadcast([128, 4, 128]), ALU.mult, (("Lx", l_i), "CBm"), (("MT", l_i),))
            if hg % 2 == 0:
                ybanks[g] = PS()
                yp, yr = ybanks[g]
                mm(yp[:], ident[:], t_sb[:, g * 512:(g + 1) * 512], True, False, ("ident", ("t_sb", g)), (yr,))
            yp, yr = ybanks[g]
            for hh in range(4):
                h = hg * 4 + hh
                hl = h % 8
                last = (hg % 2 == 1 and hh == 3)
                mm(yp[:, hl * 64:(hl + 1) * 64], MT[l_i][:, hh, :], xdt[:, h * 64:(h + 1) * 64], False, False,
                   (("MT", l_i), "xdt"), (yr,))
                mm(yp[:, hl * 64:(hl + 1) * 64], Dm[:, hh, :], xs_tok[:, bi, h * 64:(h + 1) * 64], False, last,
                   (("Dd", hg % 2, hh), "xs_tok"), (yr,))
            if hg % 2 == 1:
                gi_ = g % 2
                gt = gg[gi_]
                tt("dve", gt[:], yp[:], zs[:, bi, g * 512:(g + 1) * 512], ALU.mult, (yr, "zs"), (("gg", gi_),))
                k = rsq[0] % 8
                rsq[0] += 1
                ss = rs[:, k:k + 1]
                act(Lx[l_i][:], gt[:], AF.Square, (("gg", gi_),), (("Lx", l_i), ("rs", k)), accum=ss)
                rstd_chain(ss, 1.0 / 512, (("rs", k),), (("rs", k),))
                stt(gn[:, g * 512:(g + 1) * 512], gt[:], ss, ssdnw_b[:, g * 512:(g + 1) * 512], ALU.mult, ALU.mult,
                    (("gg", gi_), ("rs", k), "cnb"), (("gn", g),))
        for half in range(2):
            ps, pr = PS()
            pb = ps[:].bitcast(BF16)
            for c8 in range(8):
                c = half * 8 + c8
                tr(pb[:, c8 * 128:(c8 + 1) * 128], gn[:, c * 128:(c + 1) * 128], (("gn", c // 4),), (pr,))
            cp("act" if half == 0 else "dve", gnT[:, half * 8:(half + 1) * 8, bc],
               pb[:, 0:1024].rearrange("p (c t) -> p c t", c=8), (pr,), ("gnT",))

    pei = [0]

    def attn_block(bi, j):
        bc = slice(bi * 128, (bi + 1) * 128)
        prevc = slice(bi * 128, (bi + 1) * 128)
        curc = slice(128 + bi * 128, 128 + (bi + 1) * 128)
        slot = 0 if j == 1 else (1 if j == 2 else 2)
        ebv = ebias[:].rearrange("p (pc h q) -> p h pc q", pc=2, h=16)
        emv = emeta[:, slot * 2048:(slot + 1) * 2048].rearrange("p (h q) -> p h q", h=16)
        obanks = [(psums[5 + q], ("ps", 5 + q)) for q in range(3)]
        for c2 in range(4):
            psm, prm = PS()
            pts = []
            for cq in range(2):
                c = c2 * 2 + cq
                kv = c // 4
                ps, pr = PS()
                for half in range(2):
                    mm(ps[:, (half * 2) * 128:(half * 2 + 1) * 128], kT[:, kv, prevc], qT[:, half, c, bc], True, True,
                       ("kT", "qT"), (pr,))
                    mm(ps[:, (half * 2 + 1) * 128:(half * 2 + 2) * 128], kT[:, kv, curc], qT[:, half, c, bc], True, True,
                       ("kT", "qT"), (pr,))
                    mm(psm[:, (cq * 2 + half) * 128:(cq * 2 + half + 1) * 128], kmT[:, kv, :], qT[:, half, c, bc],
                       True, True, ("kmT", "qT"), (prm,))
                i = pei[0] % 2
                pei[0] += 1
                act(Pexp[i][:], ps[:], AF.Exp, (pr,), (("Pexp", i),))
                tt("dve", PT[i][:].rearrange("p (h pc q) -> p h pc q", h=2, pc=2),
                   Pexp[i][:].rearrange("p (h pc q) -> p h pc q", h=2, pc=2), ebv[:, 2 * c:2 * c + 2, :, :], ALU.mult,
                   (("Pexp", i), "ebias"), (("PT", i),))
                pts.append((PT[i], ("PT", i)))
            act(Pme[:], psm[:], AF.Exp, (prm,), ("Pme",))
            tt("dve", PTm[:].rearrange("p (h q) -> p h q", h=4), Pme[:].rearrange("p (h q) -> p h q", h=4),
               emv[:, c2 * 4:(c2 + 1) * 4, :], ALU.mult, ("Pme", "emeta"), ("PTm",))
            for cq in range(2):
                c = c2 * 2 + cq
                kv = c // 4
                pt, ptr_ = pts[cq]
                for half in range(2):
                    h = 2 * c + half
                    ob, obr = obanks[h // 7]
                    oo = ob[:, (h % 7) * 65:(h % 7) * 65 + 65]
                    mm(oo, pt[:, (half * 2) * 128:(half * 2 + 1) * 128], vaug[:, bi, kv, :], True, False,
                       (ptr_, "vaug"), (obr,))
                    mm(oo, pt[:, (half * 2 + 1) * 128:(half * 2 + 2) * 128], vaug[:, bi + 1, kv, :], False, False,
                       (ptr_, "vaug"), (obr,))
                    mm(oo, PTm[:, (cq * 2 + half) * 128:(cq * 2 + half + 1) * 128], vmeta[:, kv, :], False, True,
                       ("PTm", "vmeta"), (obr,))
        for b3 in range(3):
            h0 = b3 * 7
            nh = min(7, 16 - h0)
            ob, obr = obanks[b3]
            ov = ob[:, 0:nh * 65].rearrange("p (h d) -> p h d", d=65)
            dn = den[:, h0:h0 + nh]
            tt("dve", dn.unsqueeze(2), ov[:, :, 64:65], esink[:, h0:h0 + nh].unsqueeze(2), ALU.add,
               (obr, "small_b"), (("den", b3),))
            P.op("dve", lambda e, o=dn: e.reciprocal(o, o), (("den", b3),), (("den", b3),))
            tt("dve", att_tok[:, h0 * 64:(h0 + nh) * 64].rearrange("p (h d) -> p h d", d=64), ov[:, :, 0:64],
               dn.unsqueeze(2).to_broadcast([128, nh, 64]), ALU.mult, (obr, ("den", b3)), ("att_tok",))
        ps, pr = PS()
        pb = ps[:].bitcast(BF16)
        for c in range(8):
            tr(pb[:, c * 128:(c + 1) * 128], att_tok[:, c * 128:(c + 1) * 128], ("att_tok",), (pr,))
        cp("act", attT[:, :, bc], pb[:, 0:1024].rearrange("p (c t) -> p c t", c=8), (pr,), ("attT",))

    P.mark(1)
    dma("sp", hbuf[0][:, 0, :], xin[0], (), (("hbuf0", 0),), "x_hbuf0_0")
    dma("sp", hbuf[0][:, 1, :], meta, (), (("hbuf0", 1),), "x_hbuf0_1")
    norm_T(hbuf[0][:, 0, :], ("hbuf0", 0), nmw_b, uT, "uT", 0, "act")
    norm_T(hbuf[0][:, 1, :], ("hbuf0", 1), nmw_b, uT, "uT", 128, "dve")
    for si in range(4, 10):
        slab, sres = W.next(d_win(si))
        for cc in range(4):
            c = (si - 4) * 4 + cc
            ps, pr = fm_chunk(slab, sres, cc, uT, "uT", 128)
            cp("dve", halo0[:, c, :], ps[:, 125:128], (pr,), ("halo0",))
    slab, sres = W.next(d_win(16))
    for kv in range(2):
        ps, pr = fm_chunk(slab, sres, kv, uT, "uT", 256)
        cp("dve", kT[:, kv, 0:128], ps[:, 0:128], (pr,), ("kT",))
        cp("dve", kmT[:, kv, :], ps[:, 128:256], (pr,), ("kmT",))
    ps, pr = PS()
    for kc in range(8):
        mm(ps[:, 0:128], uT[:, kc, 0:128], slab[:, kc, 256:384], kc == 0, kc == 7, ("uT", sres), (pr,))
    ts("dve", vaug[:, 0, :, 0:64], ps[:, 0:128].rearrange("p (k d) -> p k d", k=2), cvs("vflag", 0, 1), None, ALU.mult,
       None, (pr, "cv"), ("vaug",))
    for kv in range(2):
        cp("dve", vaug[:, 0, kv, 64:65], cvs("vflag", 0, 1), ("cv",), ("vaug",))
    ps, pr = PS()
    for kc in range(8):
        mm(ps[:, 0:128], uT[:, kc, 128:256], slab[:, kc, 256:384], kc == 0, kc == 7, ("uT", sres), (pr,))
    cp("dve", vmeta[:, :, 0:64], ps[:, 0:128].rearrange("p (k d) -> p k d", k=2), (pr,), ("vmeta",))
    for kv in range(2):
        cp("dve", vmeta[:, kv, 64:65], cvs("m16"), ("cv",), ("vmeta",))

    def inproj_xbc(js, nslab, want_c):
        n = len(js) * 128
        for si in range(4, 4 + nslab):
            slab, sres = W.next(d_win(si))
            group = []
            for cc in range(4):
                c = (si - 4) * 4 + cc
                is_b = 16 <= c < 20
                is_c = c >= 20
                o = xbc_chunk(c, slab, sres, cc, n, len(js), want_tok=(c < 20), want_fm=((is_b or is_c) and want_c))
                if "fm" in o:
                    ps2, pr2 = o["fm"]
                    dst = BT if is_b else CT
                    act(dst[:, c % 4, 0:n], ps2[:, 0:n], AF.Silu, (pr2, "cv"), ("BT" if is_b else "CT",),
                        bias=cvs("convb", c, c + 1))

    def inproj_vdt(js, slab, sres, with_v):
        for bi, j in enumerate(js):
            ps, pr = PS()
            for kc in range(8):
                mm(ps[:, 0:160], uT[:, kc, bi * 128:(bi + 1) * 128], slab[:, kc, 256:416], kc == 0, kc == 7,
                   ("uT", sres), (pr,))
            if with_v:
                ts("dve", vaug[:, bi + 1, :, 0:64], ps[:, 0:128].rearrange("p (k d) -> p k d", k=2),
                   cvs("vflag", j, j + 1), None, ALU.mult, None, (pr, "cv"), ("vaug",))
                for kv in range(2):
                    cp("dve", vaug[:, bi + 1, kv, 64:65], cvs("vflag", j, j + 1), ("cv",), ("vaug",))
            tt("dve", dts[:, 10 + bi, :], ps[:, 128:160], cvs("dtb"), ALU.add, (pr, "cv"), (("dtr", bi),))

    P.mark(2)
    cp("dve", halo[:], halo0[:], ("halo0",), tuple(("halo", c) for c in range(24)))
    for gi, js in enumerate(p1groups):
        hb = hbuf[gi % 2]
        hres = f"hbuf{gi % 2}"
        load_x(js, hb, hres)
        for bi, j in enumerate(js):
            norm_T(hb[:, bi, :], (hres, bi), nmw_b, uT, "uT", bi * 128, "act" if bi % 2 == 0 else "dve")
        inproj_xbc(js, 5, False)
        slab, sres = W.next(d_win(16))
        inproj_vdt(js, slab, sres, False)
        for bi, j in enumerate(js):
            cp("dve", dts[:, 0, :], dts[:, 10 + bi, :], (("dtr", bi),), (("dts", 0),))
            dt_chain(bi, j, True)
            state_update(bi, True)

    P.mark(3)
    for g in range(4):
        dma("sp", cc_in.ap()[0:128, g * 512:(g + 1) * 512], S[:, g * 512:(g + 1) * 512], (("S", g),), ("cc_in",), "ccin")
    dma("sp", cc_in.ap()[128:129, :].rearrange("o (a b) -> (o a) b", b=32), totacc[0:64, :], ("totacc",), ("cc_in",), "ccin")
    P.bulk.add("ccin")
    P.op("pool", lambda e: e.collective_compute("AllGather", ALU.bypass, replica_groups=[list(range(8))],
                                                ins=[cc_in.ap().opt()], outs=[cc_out.ap().opt()]),
         ("cc_in",), ("cc_out",), kind="cc")
    ccv = cc_out.ap().rearrange("(r q) n -> r q n", q=129)
    dma("sp", LD[:], ccv[:, 128, 0:32].partition_broadcast(128), ("cc_out",), ("LD",), "ld")
    o_, _ = CV["sel"]
    selv = cv[:, o_:o_ + 64].rearrange("p (j m) -> p j m", j=8)
    P.op("dve", lambda e: e.memset(coef[:], 0.0), (), ("coef",))
    for m_ in range(8):
        tt("dve", prodt[:], selv[:, :, m_].unsqueeze(2).to_broadcast([128, 8, 32]),
           LD[:, m_, :].unsqueeze(1).to_broadcast([128, 8, 32]), ALU.mult, ("cv", "LD"), ("prodt",))
        tt("dve", coef[:], coef[:], prodt[:], ALU.add, ("coef", "prodt"), ("coef",))
    act(coef[:], coef[:], AF.Exp, ("coef",), ("coef",))
    tt("dve", coef[:], coef[:], cvs("maskj").unsqueeze(2).to_broadcast([128, 8, 32]), ALU.mult, ("coef", "cv"), ("coef",))
    P.op("dve", lambda e: e.memset(S[:], 0.0), tuple(("S", g) for g in range(4)), tuple(("S", g) for g in range(4)))
    for r_ in range(8):
        hb = hbuf[r_ % 2]
        hres = f"hbuf{r_ % 2}"
        hv = hb[:, 0:2, :]
        dma("sp", hv, ccv[r_, 0:128, :].rearrange("p (a b) -> p a b", a=2), ("cc_out",), ((hres, 0), (hres, 1)),
            f"xch{r_ % 2}")
        hv4 = hv.rearrange("p a (h d) -> p (a h) d", d=64)
        tt("pool", hv4, hv4, coef[:, r_, :].unsqueeze(2).to_broadcast([128, 32, 64]), ALU.mult,
           ((hres, 0), (hres, 1), "coef"), ((hres, 0), (hres, 1)))
        Sv = S[:].rearrange("p (a b) -> p a b", a=2)
        tt("dve", Sv, Sv, hv, ALU.add, tuple(("S", g) for g in range(4)) + ((hres, 0), (hres, 1)),
           tuple(("S", g) for g in range(4)))
    for g in range(4):
        cp("act", S_bf[:, g * 512:(g + 1) * 512], S[:, g * 512:(g + 1) * 512], (("S", g),), (("S_bf", g),))

    P.mark(4)
    cp("dve", halo[:], halo0[:], ("halo0",), tuple(("halo", c) for c in range(24)))
    xpi_f = [0]
    for gi, js in enumerate(p2groups):
        P.mark(5 + gi)
        P.cur_gi = gi
        full = gi > 0
        nb = len(js)
        n = nb * 128
        hb = hbuf[gi % 2]
        hres = f"hbuf{gi % 2}"
        load_x(js, hb, hres)
        for bi, j in enumerate(js):
            norm_T(hb[:, bi, :], (hres, bi), nmw_b, uT, "uT", bi * 128, "act" if bi % 2 == 0 else "dve")
        P.sub(1)
        inproj_xbc(js, 6, True)
        P.sub(2)
        for si in (10, 11):
            slab, sres = W.next(d_win(si))
            for cc in range(4):
                c = (si - 10) * 4 + cc
                ps, pr = fm_chunk(slab, sres, cc, uT, "uT", n)
                ts("dve", qT[0:64, 0, c, 0:n], ps[0:64, 0:n], 0.125, None, ALU.mult, None, (pr,), ("qT",))
                ts("dve", qT[64:128, 1, c, 0:n], ps[64:128, 0:n], 0.125, None, ALU.mult, None, (pr,), ("qT",))
        P.sub(3)
        slab, sres = W.next(d_win(16))
        for kv in range(2):
            ps, pr = fm_chunk(slab, sres, kv, uT, "uT", n)
            cp("act", kT[:, kv, 128:128 + n], ps[:, 0:n], (pr,), ("kT",))
        inproj_vdt(js, slab, sres, True)
        P.sub(4)
        for zi in range(4):
            slab, sres = W.next(d_win(zi))
            for bi in range(nb):
                ps, pr = PS()
                for kc in range(8):
                    mm(ps[:], uT[:, kc, bi * 128:(bi + 1) * 128], slab[:, kc, :], kc == 0, kc == 7, ("uT", sres), (pr,))
                act(zs[:, bi, zi * 512:(zi + 1) * 512], ps[:], AF.Silu, (pr,), ("zs",))
        P.sub(5)
        for bi, j in enumerate(js):
            cp("dve", dts[:, 0, :], dts[:, 10 + bi, :], (("dtr", bi),), (("dts", 0),))
            dt_chain(bi, j, False)
            P.sub(6)
            dbg("uT", uT[:, :, bi * 128:(bi + 1) * 128], ("uT",), j)
            dbg("xs_tok", xs_tok[:, bi, :], ("xs_tok",), j)
            dbg("B_tok", B_tok[:, bi, :], ("B_tok",), j)
            dbg("BT", BT[:, :, bi * 128:(bi + 1) * 128], ("BT",), j)
            dbg("CT", CT[:, :, bi * 128:(bi + 1) * 128], ("CT",), j)
            dbg("zs", zs[:, bi, :], ("zs",), j)
            dbg("dts", dts[:, 0:10, :], tuple(("dts", q) for q in range(10)), j)
            dbg("qT", qT[:, 0, :, bi * 128:(bi + 1) * 128], ("qT",), j)
            dbg("kT", kT[:], ("kT",), j)
            dbg("vaug", vaug[:], ("vaug",), j)
            dbg("Sin", S[:], tuple(("S", g) for g in range(4)), j)
            ssd_block(bi, j)
            P.sub(7)
            dbg("gn", gn[:], tuple(("gn", g) for g in range(4)), j)
            state_update(bi, False)
            P.sub(8)
            dbg("Sout", S[:], tuple(("S", g) for g in range(4)), j)
            attn_block(bi, j)
            P.sub(9)
            dbg("att_tok", att_tok[:], ("att_tok",), j)
        cp("pool", kT[:, :, 0:128], kT[:, :, n:n + 128], ("kT",), ("kT",))
        cp("pool", vaug[:, 0, :, :], vaug[:, nb, :, :], ("vaug",), ("vaug",))
        P.sub(10)
        for si in range(12, 16):
            slab, sres = W.next(d_win(si))
            for cc in range(4):
                o = (si - 12) * 4 + cc
                ps, pr = fm_chunk(slab, sres, cc, uT, "uT", n)
                act(tgT[:, o, 0:n], ps[:, 0:n], AF.Tanh, (pr, "small_c"), ("tgT",), bias=hgb[:, o:o + 1], scale=0.5)
        P.sub(11)
        for og in range(2):
            s0, r0 = W.next(d_ssd(0, og))
            s1, r1 = W.next(d_ssd(1, og))
            sa, ra = W.next(d_att(og))
            for cc in range(4):
                o = og * 4 + cc
                ps1, pr1 = PS()
                for c in range(16):
                    sl_, rr_ = (s0, r0) if c < 8 else (s1, r1)
                    mm(ps1[:, 0:n], sl_[:, c % 8, cc * 128:(cc + 1) * 128], gnT[:, c, 0:n], c == 0, c == 15,
                       (rr_, "gnT"), (pr1,))
                ps2, pr2 = PS()
                for c in range(8):
                    mm(ps2[:, 0:n], sa[:, c, cc * 128:(cc + 1) * 128], attT[:, c, 0:n], c == 0, c == 7, (ra, "attT"),
                       (pr2,))
                stt(m1[0][:, 0:n], tgT[:, o, 0:n], 1.0, ps1[:, 0:n], ALU.add, ALU.mult, ("tgT", pr1), (("m1", 0),))
                stt(m1[1][:, 0:n], tgT[:, 8 + o, 0:n], 1.0, ps2[:, 0:n], ALU.add, ALU.mult, ("tgT", pr2), (("m1", 1),))
                tt("dve", mT[:, o, 0:n], m1[0][:, 0:n], m1[1][:, 0:n], ALU.add, (("m1", 0), ("m1", 1)), ("mT",))
        P.sub(12)
        for hf in range(2):
            slab, sres = W.next(d_out(hf))
            for bi, j in enumerate(js):
                ps, pr = PS()
                for c in range(8):
                    mm(ps[:], mT[:, c, bi * 128:(bi + 1) * 128], slab[:, c, :], c == 0, c == 7, ("mT", sres), (pr,))
                hs = hb[:, bi, hf * 512:(hf + 1) * 512]
                stt(hs, ps[:], vh[:, j:j + 1], hs, ALU.mult, ALU.add, (pr, "small_d", (hres, bi)), ((hres, bi),))
        P.sub(13)
        for bi, j in enumerate(js):
            dbg("mT", mT[:, :, bi * 128:(bi + 1) * 128], ("mT",), j)
            dbg("hmid", hb[:, bi, :], ((hres, bi),), j)
        for bi, j in enumerate(js):
            norm_T(hb[:, bi, :], (hres, bi), nfw_b, uT, "uT", bi * 128, "act" if bi % 2 == 0 else "dve")
        for si in range(11):
            slab, sres = W.next(d_fi(si))
            for cc in range(4):
                cidx = si * 4 + cc
                ps, pr = fm_chunk(slab, sres, cc, uT, "uT", n)
                i = xpi_f[0] % NX
                xpi_f[0] += 1
                hp = hpre[i]
                hr = ("hpre", i)
                if full:
                    cp("pool", hp[:, 0:2], fhalo[:, cidx, :], ("fhalo",), (hr,))
                cp("act", hp[:, 2:2 + n], ps[:, 0:n], (pr,), (hr,))
                cp("pool", fhalo[:, cidx, :], hp[:, n:n + 2], (hr,), ("fhalo",))
                if not full:
                    continue
                o_, _ = CV["fconvw"]
                d, dr = gen_diag([cv[:, o_ + cidx * 3 + k:o_ + cidx * 3 + k + 1] for k in range(3)], 3, None)
                ps2, pr2 = PS()
                for k in range(3):
                    mm(ps2[:, 0:n], d[:, k, :], hp[:, k:k + n], k == 0, k == 2, (dr + (k,), hr), (pr2,))
                if cidx < 22:
                    act(actT[:, cidx, 0:n], ps2[:, 0:n], AF.Silu, (pr2, "cv"), (("actT", cidx),),
                        bias=cvs("fconvb", cidx, cidx + 1))
                else:
                    jx = cidx - 22
                    stt(actT[:, jx, 0:n], ps2[:, 0:n], cvs("fconvb", cidx, cidx + 1), actT[:, jx, 0:n], ALU.add,
                        ALU.mult, (pr2, "cv", ("actT", jx)), (("actT", jx),))
        if not full:
            continue
        P.sub(14)
        for bi, j in enumerate(js):
            dbg("actT", actT[:, :, bi * 128:(bi + 1) * 128], tuple(("actT", c) for c in range(22)), j)
        for hf in range(2):
            sl3 = [W.next(d_fo(kh, hf)) for kh in range(3)]
            for bi, j in enumerate(js):
                ps, pr = PS()
                for c in range(22):
                    sl_, rr_ = sl3[c // 8]
                    mm(ps[:], actT[:, c, bi * 128:(bi + 1) * 128], sl_[:, c % 8, :], c == 0, c == 21,
                       (("actT", c), rr_), (pr,))
                hs = hb[:, bi, hf * 512:(hf + 1) * 512]
                tt("dve", hs, ps[:], hs, ALU.add, (pr, (hres, bi)), ((hres, bi),))
        for bi, j in enumerate(js):
            k = rsq[0] % 8
            rsq[0] += 1
            ss = rs[:, k:k + 1]
            act(ubf[:], hb[:, bi, :], AF.Square, ((hres, bi),), ("ubf", ("rs", k)), accum=ss)
            rstd_chain(ss, 1.0 / 1024, (("rs", k),), (("rs", k),))
            stt(hb[:, bi, :], hb[:, bi, :], ss, cvs("nfin"), ALU.mult, ALU.mult, ((hres, bi), ("rs", k), "cv"),
                ((hres, bi),))
            dma("sp", yout[j - 2], hb[:, bi, :], ((hres, bi),), (("yout", hres, bi),), f"yo_{hres}_{bi}")
    P.stopped = False
    P.final = True
    fin = sb("fin", [128, 130])
    ps, pr = PS()
    mm(ps[:, 0:128], ident[:], ident[:], True, True, ("ident",), (pr,))
    cp("act", fin[:, 0:128], ps[:, 0:128], (pr,), (("fin", 0),))
    P.op("dve", lambda e: e.memset(fin[:, 128:129], 0.0), (), (("fin", 1),))
    P.op("pool", lambda e: e.memset(fin[:, 129:130], 0.0), (), (("fin", 2),))
    P.op("sp", None, tuple(("yout", f"hbuf{a}", b) for a in range(2) for b in range(G)) + tuple(("fin", q) for q in range(3)), ())

    with nc.Block() as block:
        P.emit(nc, es, block)
    es.close()
    return nc


def _t5_bucket(d):
    d = np.maximum(d, 0).astype(np.int64)
    big = 16 + (np.log(np.maximum(d, 16).astype(np.float32) / np.float32(16)) / np.float32(math.log(8.0))
                * np.float32(16)).astype(np.int32)
    return np.where(d < 16, d, np.minimum(big, 31)).astype(np.int64)


_NC_CACHE = {}


def prepare(x, meta_tokens, norm_mix_w, w_in, ssd_conv_w, ssd_conv_b, ssd_dt_bias, ssd_a_log, ssd_d, ssd_norm_w,
           w_ssd_branch, w_attn_branch, attn_sinks, rel_bias, gate_b, w_out, norm_ffn_w, w_ffn_in, ffn_conv_w,
           ffn_conv_b, w_ffn_out, norm_final_w):
    f = np.float32
    x = np.asarray(x, f)
    meta_tokens = np.asarray(meta_tokens, f)
    rel_bias = np.asarray(rel_bias, f)
    B = x.shape[0]
    idx = np.arange(128)
    cst = np.concatenate([np.eye(128, dtype=f), (idx[:, None] <= idx[None, :]).astype(f),
                          (idx[:, None] > idx[None, :]).astype(f), np.ones((128, 128), f)], axis=1)
    bc = lambda v: np.broadcast_to(np.asarray(v, f).reshape(1, -1), (128, np.asarray(v).size))
    cnorm = np.ascontiguousarray(np.concatenate([bc(norm_mix_w[0]), bc(norm_ffn_w[0]), bc(ssd_norm_w[0])], axis=1))
    cw = np.asarray(ssd_conv_w[0], f)
    convw_fm = cw.T.reshape(24, 128, 4).transpose(1, 0, 2).reshape(128, 96)
    convb_fm = np.asarray(ssd_conv_b[0], f).reshape(24, 128).T
    fw = np.asarray(ffn_conv_w[0], f)
    fw2 = np.concatenate([fw[:, 2816:], fw[:, :2816]], axis=1)
    fconvw_fm = fw2.T.reshape(44, 128, 3).transpose(1, 0, 2).reshape(128, 132)
    fb = np.asarray(ffn_conv_b[0], f)
    fb2 = np.concatenate([fb[2816:], fb[:2816]])
    fconvb_fm = fb2.reshape(44, 128).T
    gateb_fm = np.asarray(gate_b[0], f).reshape(16, 128).T
    s_ = idx[:, None]
    q_ = idx[None, :]
    dprev = q_ - s_ + 128
    dcur = q_ - s_
    bandb = np.zeros((128, 2, 16, 128), f)
    bandm = np.zeros((128, 2, 16, 128), f)
    bandb[:, 0] = rel_bias[_t5_bucket(dprev)].transpose(0, 2, 1)
    bandb[:, 1] = rel_bias[_t5_bucket(dcur)].transpose(0, 2, 1)
    bandm[:, 0] = ((dprev >= 0) & (dprev < 128))[:, None, :]
    bandm[:, 1] = ((dcur >= 0) & (dcur < 128))[:, None, :]

    def meta_tab(gb):
        m_ = np.arange(16)[:, None]
        qq = np.arange(128)[None, :]
        dist = gb * 128 + qq - (112 + m_)
        tb = rel_bias[_t5_bucket(dist)].transpose(0, 2, 1)
        tm = np.broadcast_to((dist >= 0)[:, None, :], (16, 16, 128))
        return tb.astype(f), tm.astype(f)

    meta_tile = np.zeros((128, 1024), f)
    meta_tile[:16] = meta_tokens
    blk0 = np.zeros((128, 1024), f)
    blk0[112:] = meta_tokens
    in_maps = []
    for r in range(8):
        b, k = divmod(r, 4)
        blocks = []
        for j in range(NBLK):
            gb = 16 * k - 1 + j
            if gb < 0:
                blocks.append(np.zeros((128, 1024), f))
            elif gb == 0:
                blocks.append(blk0)
            else:
                blocks.append(x[b, (gb - 1) * 128:gb * 128])
        xin = np.ascontiguousarray(np.stack(blocks))
        vflag = np.ones(18, f)
        valid = np.ones((128, 18), f)
        if k == 0:
            vflag[0] = 0
            vflag[1] = 0
            valid[:, 0] = 0
            valid[:112, 1] = 0
        sel = np.zeros((8, 8), f)
        maskj = np.zeros(8, f)
        for kk in range(k):
            jj = 4 * b + kk
            maskj[jj] = 1
            for m in range(jj + 1, r):
                sel[jj, m] = 1
        tabs = [meta_tab(16 * k - 1 + 1), meta_tab(16 * k - 1 + 2), meta_tab(40)]
        metab = np.zeros((128, 6144), f)
        metam = np.zeros((128, 6144), f)
        metab[:16] = np.concatenate([t[0].reshape(16, 2048) for t in tabs], axis=1)
        metam[:16] = np.concatenate([t[1].reshape(16, 2048) for t in tabs], axis=1)
        parts = {"nfin": bc(norm_final_w), "dtb": bc(ssd_dt_bias[0]), "alog": bc(ssd_a_log[0]), "dskip": bc(ssd_d[0]),
                 "sinks": bc(attn_sinks[0]), "convw": convw_fm, "convb": convb_fm, "fconvw": fconvw_fm,
                 "fconvb": fconvb_fm, "gateb": gateb_fm, "vflag": bc(vflag), "valid": valid, "sel": bc(sel.reshape(-1)),
                 "maskj": bc(maskj), "m16": (np.arange(128) < 16).astype(f).reshape(128, 1)}
        cvec = np.zeros((128, NV), f)
        for nme, (o, w) in CV.items():
            cvec[:, o:o + w] = parts[nme]
        in_maps.append({
            "xin": xin, "meta": meta_tile, "w_in": np.ascontiguousarray(np.asarray(w_in[0], f)),
            "w_ssd": np.ascontiguousarray(np.asarray(w_ssd_branch[0], f)),
            "w_att": np.ascontiguousarray(np.asarray(w_attn_branch[0], f)),
            "w_out": np.ascontiguousarray(np.asarray(w_out[0], f)),
            "w_fi": np.ascontiguousarray(np.asarray(w_ffn_in[0], f)),
            "w_fo": np.ascontiguousarray(np.asarray(w_ffn_out[0], f)),
            "cst": np.ascontiguousarray(cst), "cnorm": cnorm, "cvec": cvec,
            "bandb": np.ascontiguousarray(bandb.reshape(128, 4096)), "bandm": np.ascontiguousarray(bandm.reshape(128, 4096)),
            "metab": np.ascontiguousarray(metab), "metam": np.ascontiguousarray(metam),
        })
    return in_maps


def kernel(**inputs):
    in_maps = prepare(**inputs)
    x = inputs["x"]
    B = x.shape[0]
    f = np.float32
    if "nc" not in _NC_CACHE:
        _NC_CACHE["nc"] = build_nc()
    res = run_bass_kernel_spmd(_NC_CACHE["nc"], in_maps, core_ids=list(range(8)))
    out = np.zeros((B, x.shape[1], 1024), f)
    for r in range(8):
        b, k = divmod(r, 4)
        out[b, k * 2048:(k + 1) * 2048] = np.asarray(res.results[r]["y"], f).reshape(2048, 1024)
    return out
```
